# Optimizing a Trainium2 kernel written in Bass

```python
import jax, jax.numpy as jnp
from jax import lax
import numpy as np

D_MODEL = 1024
BATCH = 4
SEQ = 8192
DEPTH = 2

CHUNK = 64
D_BRANCH = 768
N_BRANCH = 3
CONV_K = 31
GMLP_BLOCK = 128
GMLP_HEADS = 4
GMLP_HD = D_BRANCH // GMLP_HEADS
POOL_WINDOWS = (2, 4, 8, 16)
POOL_GROUPS = 4
POOL_GD = D_BRANCH // POOL_GROUPS
D_FF = 2 * D_MODEL
FFN_K = 3
EPS = 1e-6
OFF_CONV = 0
OFF_GMLP = 2 * D_BRANCH
OFF_POOL = 4 * D_BRANCH
OFF_GATE = 5 * D_BRANCH
N_IN = 5 * D_BRANCH + N_BRANCH * D_MODEL

kernel_name = "hybrid_conv_gmlp_pool_encoder"


def rmsnorm(x, g):
    xf = x.astype(jnp.float32)
    y = xf * lax.rsqrt(jnp.mean(xf * xf, axis=-1, keepdims=True) + EPS)
    return (y * g.astype(jnp.float32)).astype(x.dtype)


def layernorm(x, g, b):
    xf = x.astype(jnp.float32)
    mu = jnp.mean(xf, axis=-1, keepdims=True)
    var = jnp.mean(jnp.square(xf - mu), axis=-1, keepdims=True)
    y = (xf - mu) * lax.rsqrt(var + EPS)
    return (y * g.astype(jnp.float32) + b.astype(jnp.float32)).astype(x.dtype)


def dwconv_causal(x, w):
    k, c = w.shape
    filt = w.astype(x.dtype)[:, None, :]
    return lax.conv_general_dilated(
        x, filt, window_strides=(1,), padding=[(k - 1, 0)],
        dimension_numbers=("NWC", "WIO", "NWC"), feature_group_count=c)


def multiscale_pool(p, pool_w, pool_scale):
    b, s, _ = p.shape
    pf = p.astype(jnp.float32)
    c = jnp.cumsum(pf, axis=1)
    pos = (jnp.arange(s) + 1)[None, :, None]
    outs = []
    for g, w in enumerate(POOL_WINDOWS):
        cg = c[..., g * POOL_GD:(g + 1) * POOL_GD]
        prev = jnp.pad(cg[:, :-w], ((0, 0), (w, 0), (0, 0)))
        cnt = jnp.minimum(pos, w).astype(jnp.float32)
        outs.append((cg - prev) / cnt)
    pooled = (jnp.concatenate(outs, axis=-1) - pf).astype(p.dtype)
    pooled = pooled.reshape(b, s, POOL_GROUPS, POOL_GD)
    mixed = jnp.einsum("bsgc,gcd->bsgd", pooled, pool_w).reshape(b, s, D_BRANCH)
    return mixed * pool_scale


def hybrid_mixer(xn, w_in, b_in, conv_w, conv_ln_g, conv_ln_b, conv_proj,
                 gmlp_ln_g, gmlp_ln_b, gmlp_ws, gmlp_bs, gmlp_proj,
                 pool_w, pool_scale, pool_proj, w_out):
    b, s, _ = xn.shape
    z = xn @ w_in + b_in
    zc = z[..., OFF_CONV:OFF_GMLP]
    zg = z[..., OFF_GMLP:OFF_POOL]
    zp = z[..., OFF_POOL:OFF_GATE]
    zgate = z[..., OFF_GATE:]

    a = zc[..., :D_BRANCH] * jax.nn.sigmoid(zc[..., D_BRANCH:])
    a = dwconv_causal(a, conv_w)
    a = jax.nn.silu(layernorm(a, conv_ln_g, conv_ln_b))
    h_a = a @ conv_proj

    zg = jax.nn.gelu(zg)
    u, v = zg[..., :D_BRANCH], zg[..., D_BRANCH:]
    v = layernorm(v, gmlp_ln_g, gmlp_ln_b)
    v = v.reshape(b, s // GMLP_BLOCK, GMLP_BLOCK, GMLP_HEADS, GMLP_HD)
    blk = jnp.arange(GMLP_BLOCK) // CHUNK
    mask = (blk[None, :] <= blk[:, None]).astype(gmlp_ws.dtype)
    ws = gmlp_ws * mask[None]
    sv = jnp.einsum("hij,bnjhd->bnihd", ws, v) + gmlp_bs.T[:, :, None]
    h_b = (u * sv.reshape(b, s, D_BRANCH)) @ gmlp_proj

    h_c = multiscale_pool(zp, pool_w, pool_scale) @ pool_proj

    gates = jax.nn.sigmoid(zgate).reshape(b, s, N_BRANCH, D_MODEL)
    y = gates[..., 0, :] * h_a + gates[..., 1, :] * h_b + gates[..., 2, :] * h_c
    return y @ w_out


def conv_ffn(xn, w_up, ffn_conv_w, w_down):
    h = xn @ w_up
    h = dwconv_causal(h, ffn_conv_w)
    g, val = h[..., :D_FF], h[..., D_FF:]
    return (jax.nn.gelu(g) * val) @ w_down


def setup_inputs(seed: int = 0) -> dict:
    key = jax.random.key(seed)
    ks = iter(jax.random.split(key, 32))
    f32 = jnp.float32

    def nrm(shape, scale):
        return jax.random.normal(next(ks), shape, f32) * scale

    L = DEPTH
    return {
        "x": nrm((BATCH, SEQ, D_MODEL), 1.0),
        "attn_norm_g": 1.0 + nrm((L, D_MODEL), 0.02),
        "w_in": nrm((L, D_MODEL, N_IN), D_MODEL ** -0.5),
        "b_in": nrm((L, N_IN), 0.02),
        "conv_w": nrm((L, CONV_K, D_BRANCH), CONV_K ** -0.5),
        "conv_ln_g": 1.0 + nrm((L, D_BRANCH), 0.02),
        "conv_ln_b": nrm((L, D_BRANCH), 0.02),
        "conv_proj": nrm((L, D_BRANCH, D_MODEL), D_BRANCH ** -0.5),
        "gmlp_ln_g": 1.0 + nrm((L, D_BRANCH), 0.02),
        "gmlp_ln_b": nrm((L, D_BRANCH), 0.02),
        "gmlp_ws": nrm((L, GMLP_HEADS, GMLP_BLOCK, GMLP_BLOCK), GMLP_BLOCK ** -0.5),
        "gmlp_bs": 1.0 + nrm((L, GMLP_HEADS, GMLP_BLOCK), 0.02),
        "gmlp_proj": nrm((L, D_BRANCH, D_MODEL), D_BRANCH ** -0.5),
        "pool_w": nrm((L, POOL_GROUPS, POOL_GD, POOL_GD), POOL_GD ** -0.5),
        "pool_scale": 1.0 + nrm((L, D_BRANCH), 0.02),
        "pool_proj": nrm((L, D_BRANCH, D_MODEL), D_BRANCH ** -0.5),
        "w_out": nrm((L, D_MODEL, D_MODEL), D_MODEL ** -0.5),
        "ffn_norm_g": 1.0 + nrm((L, D_MODEL), 0.02),
        "w_up": nrm((L, D_MODEL, 2 * D_FF), D_MODEL ** -0.5),
        "ffn_conv_w": nrm((L, FFN_K, 2 * D_FF), FFN_K ** -0.5),
        "w_down": nrm((L, D_FF, D_MODEL), D_FF ** -0.5),
        "final_norm_g": 1.0 + nrm((D_MODEL,), 0.02),
    }


def reference(x, attn_norm_g, w_in, b_in, conv_w, conv_ln_g, conv_ln_b, conv_proj,
              gmlp_ln_g, gmlp_ln_b, gmlp_ws, gmlp_bs, gmlp_proj,
              pool_w, pool_scale, pool_proj, w_out,
              ffn_norm_g, w_up, ffn_conv_w, w_down, final_norm_g):
    for l in range(DEPTH):
        h = rmsnorm(x, attn_norm_g[l])
        x = x + hybrid_mixer(h, w_in[l], b_in[l], conv_w[l], conv_ln_g[l], conv_ln_b[l],
                             conv_proj[l], gmlp_ln_g[l], gmlp_ln_b[l], gmlp_ws[l],
                             gmlp_bs[l], gmlp_proj[l], pool_w[l], pool_scale[l],
                             pool_proj[l], w_out[l])
        h = rmsnorm(x, ffn_norm_g[l])
        x = x + conv_ffn(h, w_up[l], ffn_conv_w[l], w_down[l])
    return rmsnorm(x, final_norm_g)
```

```python
import numpy as np
import concourse.bass as bass
import concourse.mybir as mybir
from concourse.bass_utils import run_bass_kernel_spmd

F32 = mybir.dt.float32
BF16 = mybir.dt.bfloat16
AF = mybir.ActivationFunctionType
ALU = mybir.AluOpType

D = 1024
DB = 768
NIN = 6912
DFF = 2048
EPS = 1e-6
HALO = 256
TOK = 4096
TT = 512
CK = 31
NPV = 378
SLOT = 8192
NS = 6


class Buf:
    __slots__ = ("name", "w", "r", "alias", "lo", "hi", "excl")

    def __init__(self, name, lo=None, hi=None, excl=False):
        self.excl = excl
        self.name = name
        self.w = None
        self.r = {}
        self.alias = []
        self.lo = lo
        self.hi = hi


class Sync:
    def __init__(self, nc):
        self.nc = nc
        self.E = {"pe": nc.tensor, "act": nc.scalar, "dve": nc.vector, "pool": nc.gpsimd, "sp": nc.sync}
        self.sems = {}
        self.cnt = {}
        self.waited = {e: {} for e in self.E}
        self.nwait = 0
        for e in self.E:
            self.newsem(e)

    def newsem(self, key):
        self.sems[key] = self.nc.alloc_semaphore(name="s_" + key)
        self.cnt[key] = 0

    def _deps(self, reads, writes):
        d = []
        for b in reads:
            if b.w is not None:
                d.append(b.w)
        for b in writes:
            for b2 in [b] + b.alias:
                if b2.w is not None:
                    d.append(b2.w)
                d.extend(b2.r.items())
        return d

    def _wait(self, eng, deps):
        for k, v in deps:
            if k == eng and eng == "pe":
                continue
            if self.waited[eng].get(k, 0) < v:
                self.E[eng].wait_ge(self.sems[k], v)
                self.waited[eng][k] = v
                self.nwait += 1

    def _commit(self, t, reads, writes):
        for b in reads:
            b.r[t[0]] = max(b.r.get(t[0], 0), t[1])
        for b in writes:
            for b2 in [b] + b.alias:
                b2.w = t
                b2.r = {}

    @staticmethod
    def _split(reads, writes):
        ex = [b for b in reads if b.excl]
        if not ex:
            return reads, writes
        return [b for b in reads if not b.excl], list(writes) + ex

    def op(self, eng, fn, reads=(), writes=()):
        reads, writes = self._split(reads, writes)
        self._wait(eng, self._deps(reads, writes))
        ins = fn()
        self.cnt[eng] += 1
        ins.then_inc(self.sems[eng], 1)
        self._commit((eng, self.cnt[eng]), reads, writes)

    def group(self, eng, fns, reads=(), writes=()):
        reads, writes = self._split(reads, writes)
        self._wait(eng, self._deps(reads, writes))
        ins = None
        for f in fns:
            ins = f()
        self.cnt[eng] += 1
        ins.then_inc(self.sems[eng], 1)
        self._commit((eng, self.cnt[eng]), reads, writes)

    def dma(self, q, semkey, fn, reads=(), writes=()):
        if semkey is None:
            semkey = f"one{len(self.sems)}"
            self.newsem(semkey)
        self._wait(q, self._deps(reads, writes))
        ins = fn()
        self.cnt[semkey] += 16
        ins.then_inc(self.sems[semkey], 16)
        self._commit((semkey, self.cnt[semkey]), reads, writes)


class _Stop(Exception):
    pass


_STAGE_LIMIT = 10 ** 9
_TINY = False
_STAGE_TILE = 0


class Prog:
    def chk(self, stage):
        if stage >= _STAGE_LIMIT and getattr(self, "ti", 0) >= _STAGE_TILE:
            raise _Stop()

    def __init__(self, n_main=8, layers=(0, 1), do_final=True, dbg=False):
        self.n_main = n_main
        self.layers = layers
        self.do_final = do_final
        self.dbg = dbg
        self.ntok_in = HALO + n_main * TT
        self.ntok_out = n_main * TT
        nc = bass.Bass("TRN2", target_bir_lowering=False)
        self.nc = nc
        self.S = Sync(nc)
        self.allbufs = []
        self.build()

    def alloc(self, name, shape, dtype, nchunk=None, off=None):
        nc = self.nc
        esz = 4 if dtype == F32 else 2
        size = int(np.prod(shape[1:])) * esz
        if off is None:
            off = self.cur
            self.cur += (size + 63) // 64 * 64
            assert self.cur <= self.top, (name, self.cur, self.top)
        t = nc.alloc_sbuf_tensor_at(name, list(shape), dtype, offset=off)
        n = nchunk or 1
        bufs = []
        for i in range(n):
            b = Buf(f"{name}{i}", off + i * size // n, off + (i + 1) * size // n)
            for o in self.allbufs:
                if o.lo < b.hi and b.lo < o.hi:
                    o.alias.append(b)
                    b.alias.append(o)
            bufs.append(b)
        self.allbufs.extend(bufs)
        return t, bufs

    def bank(self):
        i = self.bank_i
        self.bank_i = (i + 1) % 8
        return self.ps[i], self.psb[i]

    def piece(self, q, src, parts, kc, ncols):
        self.pieces.append((q, src, parts, kc, ncols))
        return len(self.pieces) - 1

    def issue_upto(self, idx):
        nc, S = self.nc, self.S
        idx = min(idx, len(self.pieces) - 1)
        while self.issued < idx:
            self.issued += 1
            i = self.issued
            q, src, parts, kc, ncols = self.pieces[i]
            s = i % NS
            dst = self.slot[s][0:parts, 0:kc * ncols].rearrange("p (k n) -> p k n", k=kc)
            eng = nc.gpsimd if q == "pool" else nc.sync
            rd = []
            S.dma(q, f"slot{s}{q}", (lambda eng=eng, dst=dst, src=src: eng.dma_start(out=dst, in_=src)),
                  reads=rd, writes=[self.slotb[s]])

    def use(self, idx, look=3):
        self.issue_upto(idx + look)
        q, src, parts, kc, ncols = self.pieces[idx]
        s = idx % NS
        v = self.slot[s][0:parts, 0:kc * ncols].rearrange("p (k n) -> p k n", k=kc)
        return v, self.slotb[s]

    def build(self):
        nc, S = self.nc, self.S
        L = 2
        dt = nc.dram_tensor
        self.x_d = dt("x", [self.ntok_in, D], F32, kind="ExternalInput").ap()
        if _TINY:
            D_, NIN_, DB_, DFF_ = 128, 128, 96, 128
        else:
            D_, NIN_, DB_, DFF_ = D, NIN, DB, DFF
        self.w_in_d = dt("w_in", [L, D_, NIN_], F32, kind="ExternalInput").ap()
        self.b_in_d = dt("b_in", [L, NIN], F32, kind="ExternalInput").ap()
        self.cproj_d = dt("conv_proj", [L, DB_, D_], F32, kind="ExternalInput").ap()
        self.gproj_d = dt("gmlp_proj", [L, DB_, D_], F32, kind="ExternalInput").ap()
        self.pproj_d = dt("pool_proj", [L, DB_, D_], F32, kind="ExternalInput").ap()
        self.wout_d = dt("w_out", [L, D_, D_], F32, kind="ExternalInput").ap()
        self.wup_d = dt("w_up", [L, D_, 2 * DFF_], F32, kind="ExternalInput").ap()
        self.wdown_d = dt("w_down", [L, DFF_, D_], F32, kind="ExternalInput").ap()
        self.ws_d = dt("gmlp_ws", [L, 4, 128, 128], F32, kind="ExternalInput").ap()
        self.poolw_d = dt("pool_w", [L, 4, 192, 192], F32, kind="ExternalInput").ap()
        self.pv_d = dt("pv", [L, 128, NPV], F32, kind="ExternalInput").ap()
        self.bsb_d = dt("bsb", [L, 128, 4, 128], F32, kind="ExternalInput").ap()
        self.gf_d = dt("gfb", [128, D], F32, kind="ExternalInput").ap()
        self.cst_d = dt("cst", [128, 2, 128], F32, kind="ExternalInput").ap()
        self.band_d = dt("band", [128, 16, 128], F32, kind="ExternalInput").ap()
        self.mask_d = dt("mask", [128, 1], F32, kind="ExternalInput").ap()
        self.y_d = dt("y", [self.ntok_out, D], F32, kind="ExternalOutput").ap()
        self.d1_d = dt("d1s", [L, 128, 6 * CK, 128], BF16).ap()
        self.d2_d = dt("d2s", [L, 128, 96, 128], BF16).ap()
        self.d_scratch_buf = Buf("dscratch")
        if self.dbg:
            self.dbg_d = dt("dbg", [8, 128, 8 * TT], F32, kind="ExternalOutput").ap()

        self.cur = (nc.sbuf_base + 63) // 64 * 64
        self.top = nc.sbuf_top
        A = self.alloc
        self.xT, self.xTb = A("xT", [128, 8, TT], F32, 8)
        self.hT, self.hTb = A("hT", [128, 8, TT], BF16, 8)
        self.sqb, self.sqbb = A("sqb", [128, 12, TT], BF16, 12)
        self.rt, self.rtb = A("rt", [128, 3, TT], F32, 3)
        self.identf, self.identfb = A("identf", [128, 128], F32)
        self.identb, self.identbb = A("identb", [128, 128], BF16)
        self.onesb, self.onesbb = A("onesb", [128, 128], BF16)
        self.onesf, self.onesfb = A("onesf", [128, 128], F32)
        self.gmask, self.gmaskb = A("gmask", [128, 128], F32)
        self.band, self.bandb = A("band", [128, 16, 128], BF16)
        self.maskt, self.masktb = A("maskt", [128, 1], F32)
        self.gfb, self.gfbb = A("gfb", [128, D], F32)
        self.pv, self.pvb = A("pv", [128, L, NPV], F32)
        self.pvh, self.pvhb = A("pvh", [128, L, NPV], F32)
        self.brow, self.browb = A("brow", [1, L, 2 * DB], BF16)
        self.Cm, self.Cmb = A("Cm", [96, L, 8, 128], F32)
        self.wsT, self.wsTb = A("wsT", [128, L, 4, 128], BF16)
        self.poolw, self.poolwb = A("poolw", [96, L, 8, 192], BF16)
        self.ahist, self.ahistb = A("ahist", [128, L, 6, 32], BF16)
        self.fhist, self.fhistb = A("fhist", [128, L, 2, 32, 2], BF16, 4)
        self.zprev, self.zprevb = A("zprev", [128, L, DB], BF16)
        self.stat, self.statb = A("stat", [128, 32], F32)
        self.slot, self.slotb = [], []
        for s in range(NS):
            t, b = A(f"slot{s}", [128, SLOT // 2], BF16)
            self.slot.append(t)
            self.slotb.append(b[0])
            S.newsem(f"slot{s}pool")
            S.newsem(f"slot{s}sp")
        mix0 = self.cur
        self.abuf, self.abufb = A("abuf", [128, 6, 32 + TT], BF16, 6)
        self.cbuf, self.cbufb = A("cbuf", [128, 6, TT], F32, 6)
        self.tmpA, self.tmpAb = A("tmpA", [128, 4, TT], F32, 4)
        self.sT, self.sTb = A("sT", [128, 6, TT], BF16, 6)
        self.uT, self.uTb = A("uT", [96, 8, TT], BF16, 8)
        self.vn, self.vnb = A("vn", [128, 4, DB], BF16, 4)
        self.tmpB, self.tmpBb = A("tmpB", [128, 2, TT], F32, 2)
        gv_off = self.cur
        self.zpt, self.zptb = A("zpt", [128, 5, DB], BF16, 5)
        self.pooledT, self.pooledTb = A("pooledT", [96, 8, TT], BF16, 8)
        self.yT, self.yTb = A("yT", [128, 8, TT], BF16, 8)
        mix1 = self.cur
        self.gv, self.gvb = A("gv", [128, 4, DB], F32, 4, off=gv_off)
        self.yacc, self.yaccb = A("yacc", [128, 8, TT], F32, 8, off=self.cbuf_off())
        self.thg, self.thgb = A("thg", [128, 2, TT], F32, 2, off=self.off_of("tmpB"))
        self.tg, self.tgb = A("tg", [128, 2, TT], F32, 2, off=self.off_of("vn"))
        self.xstage, self.xstageb = A("xstage", [128, 4, D], F32, 4, off=self.off_of("pooledT") - 0)
        self.ostage, self.ostageb = A("ostage", [128, 2, D], F32, 2, off=self.off_of("uT"))
        self.cur = mix0
        self.actT, self.actTb = A("actT", [128, 16, TT], BF16, 16)
        self.hup, self.hupb = A("hup", [128, 4, TT], BF16, 4)
        self.gg, self.ggb = A("gg", [128, 2, TT], F32, 2)
        assert self.cur <= mix1
        self.cur = mix1
        self.dst_, self.dstb = A("dstg", [128, 2, 32, 128], BF16, 2, off=self.off_of("cbuf"))
        self.wsst, self.wsstb = A("wsst", [128, 4, 128], F32, off=self.off_of("sT"))
        self.wstf, self.wstfb = A("wstf", [128, 4, 128], F32, off=self.off_of("uT"))
        self.bsbt, self.bsbtb = A("bsbt", [128, 4, 128], F32, off=self.off_of("vn"))

        self.ps, self.psb = [], []
        for i in range(8):
            t = nc.alloc_psum_tensor(f"ps{i}", [128, 512], F32)
            self.ps.append(t)
            self.psb.append(Buf(f"ps{i}", excl=True))
        self.bank_i = 0
        for k in ("xin", "yout0", "yout1", "misc", "scr0", "scr1"):
            S.newsem(k)

        self.pieces = []
        self.issued = -1
        self.setup()
        if not _TINY:
            self.plan_pieces()
        tiles = [(0, HALO)] + [(HALO + i * TT, TT) for i in range(self.n_main)]
        try:
            self.chk(0)
            self.load_x(0)
            self.main_loop(tiles)
        except _Stop:
            pass
        for k in ("yout0", "yout1"):
            if S.cnt[k] > 0:
                nc.sync.wait_ge(S.sems[k], S.cnt[k])
        for e in ("pe", "act", "dve"):
            if S.cnt[e] > 0:
                nc.sync.wait_ge(S.sems[e], S.cnt[e])
        for k, v in S.cnt.items():
            if k.startswith("slot") or k == "xin":
                if v > 0:
                    nc.sync.wait_ge(S.sems[k], v)

    def main_loop(self, tiles):
        nc, S = self.nc, self.S
        for ti, (t0, T) in enumerate(tiles):
            self.T = T
            self.nb = T // 128
            self.ti = ti
            self.transpose_in()
            self.chk(1)
            if ti + 1 < len(tiles):
                self.next_tile = ti + 1
            else:
                self.next_tile = None
            self.tiles = tiles
            for l in self.layers:
                self.mixer(l)
                self.ffn(l)
            if ti >= 1 and self.do_final:
                self.final(t0 - HALO)
            elif self.next_tile is not None and not self.x_loaded_for == self.next_tile:
                self.load_x(self.next_tile)

    def off_of(self, name):
        for b in self.allbufs:
            if b.name == name + "0":
                return b.lo
        raise KeyError(name)

    def cbuf_off(self):
        return self.off_of("cbuf")

    def setup(self):
        nc, S = self.nc, self.S
        L = 2
        sp = nc.sync
        S.dma("sp", None, lambda: sp.dma_start(out=self.identf[:, :], in_=self.cst_d[:, 0, :]), writes=self.identfb)
        S.dma("sp", None, lambda: sp.dma_start(out=self.gmask[:, :], in_=self.cst_d[:, 1, :]), writes=self.gmaskb)
        S.dma("sp", None, lambda: sp.dma_start(out=self.maskt[:, :], in_=self.mask_d), writes=self.masktb)
        S.dma("sp", None, lambda: sp.dma_start(out=self.gfb[:, :], in_=self.gf_d), writes=self.gfbb)
        S.dma("sp", None, lambda: sp.dma_start(out=self.pv[:, :, :], in_=self.pv_d.rearrange("l p n -> p l n")),
              writes=self.pvb)
        S.dma("pool", None, lambda: nc.gpsimd.dma_start(out=self.band[:, :, :], in_=self.band_d), writes=self.bandb)
        S.dma("pool", None, lambda: nc.gpsimd.dma_start(out=self.identb[:, :], in_=self.cst_d[:, 0, :]),
              writes=self.identbb)
        for l in range(L):
            S.dma("pool", None, lambda l=l: nc.gpsimd.dma_start(out=self.brow[0:1, l, :],
                                                                 in_=self.b_in_d[l:l + 1, 2304:3840]),
                  writes=self.browb)
            S.dma("pool", None, lambda l=l: nc.gpsimd.dma_start(
                out=self.poolw[:, l, :, :], in_=self.poolw_d[l].rearrange("g (k p) n -> p (g k) n", p=96)),
                writes=self.poolwb)
        if _STAGE_LIMIT <= -5:
            return
        S.op("dve", lambda: nc.vector.memset(self.onesb[:, :], 1.0), writes=self.onesbb)
        S.op("dve", lambda: nc.vector.memset(self.onesf[:, :], 1.0), writes=self.onesfb)
        S.op("dve", lambda: nc.vector.memset(self.ahist[:, :, :, :].rearrange("p a b c -> p (a b c)"), 0.0), writes=self.ahistb)
        S.op("dve", lambda: nc.vector.memset(self.fhist[:, :, :, :, :].rearrange("p a b c d -> p (a b c d)"), 0.0), writes=self.fhistb)
        S.op("dve", lambda: nc.vector.memset(self.zprev[:, :, :].rearrange("p a b -> p (a b)"), 0.0), writes=self.zprevb)
        S.op("dve", lambda: nc.vector.tensor_scalar(out=self.pvh[:, :, :].rearrange("p a b -> p (a b)"),
                                                    in0=self.pv[:, :, :].rearrange("p a b -> p (a b)"), scalar1=0.5,
                                                    scalar2=None, op0=ALU.mult),
             reads=self.pvb, writes=self.pvhb)
        if _STAGE_LIMIT <= -4:
            return
        for l in range(L):
            S.dma("sp", None, lambda l=l: sp.dma_start(out=self.wsst[:, :, :],
                                                         in_=self.ws_d[l].rearrange("h i j -> i h j")),
                  writes=self.wsstb)
            S.dma("sp", None, lambda l=l: sp.dma_start(out=self.bsbt[:, :, :], in_=self.bsb_d[l]),
                  writes=self.bsbtb)
            for h in range(4):
                S.op("dve", lambda h=h: nc.vector.tensor_tensor(out=self.wsst[:, h, :], in0=self.wsst[:, h, :],
                                                                in1=self.gmask[:, :], op=ALU.mult),
                     reads=self.wsstb + self.gmaskb, writes=self.wsstb)
            pst, psb = self.bank()
            S.group("pe", [(lambda h=h: nc.tensor.transpose(pst[:, h * 128:(h + 1) * 128], self.wsst[:, h, :],
                                                            self.identf[:, :])) for h in range(4)],
                    reads=self.wsstb + self.identfb, writes=[psb])
            S.op("act", lambda l=l: nc.scalar.copy(out=self.wsT[:, l, :, :],
                                                   in_=pst[:, :].rearrange("p (h i) -> p h i", h=4)),
                 reads=[psb], writes=self.wsTb)
            S.op("dve", lambda: nc.vector.tensor_copy(out=self.wstf[:, :, :],
                                                      in_=pst[:, :].rearrange("p (h i) -> p h i", h=4)),
                 reads=[psb], writes=self.wstfb)
            if _STAGE_LIMIT <= -3:
                continue
            pr, prb = self.bank()
            S.group("pe", [lambda: nc.tensor.matmul(pr[:, :], lhsT=self.onesf[:, :],
                                                    rhs=self.wstf[:, :, :].rearrange("p h i -> p (h i)"),
                                                    start=True, stop=True)],
                    reads=self.wstfb + self.onesfb, writes=[prb])
            GLB = 8 + 8 + 6 + 6 + 8 + 24 + 6 + 6 + 8
            for c in range(8):
                h = c // 2
                S.op("dve", lambda l=l, c=c, h=h: nc.vector.scalar_tensor_tensor(
                    out=self.Cm[:, l, c, :], in0=pr[0:96, h * 128:(h + 1) * 128],
                    scalar=self.pv[0:96, l, GLB + c:GLB + c + 1], in1=self.bsbt[0:96, h, :],
                    op0=ALU.mult, op1=ALU.add), reads=[prb] + self.pvb + self.bsbtb, writes=self.Cmb)
            if _STAGE_LIMIT <= -2:
                continue
            CW = GLB + 8 + 8
            FW = CW + 6 * CK
            for c in range(6):
                sl = c % 2
                for k in range(CK):
                    S.op("dve", lambda l=l, c=c, k=k, sl=sl: nc.vector.tensor_scalar(
                        out=self.dst_[:, sl, k, :], in0=self.identb[:, :],
                        scalar1=self.pvh[:, l, CW + c * CK + k:CW + c * CK + k + 1], scalar2=None, op0=ALU.mult),
                        reads=self.identbb + self.pvhb, writes=[self.dstb[sl]])
                S.dma("sp", f"scr{sl}", lambda l=l, c=c, sl=sl: sp.dma_start(
                    out=self.d1_d[l, :, c * CK:(c + 1) * CK, :], in_=self.dst_[:, sl, 0:CK, :]),
                    reads=[self.dstb[sl]])
            for q in range(4):
                sl = q % 2
                chunks = [4 * q + jj for jj in range(4)] + [16 + 4 * q + jj for jj in range(4)]
                for ci, j in enumerate(chunks):
                    for k in range(3):
                        S.op("dve", lambda l=l, j=j, k=k, sl=sl, ci=ci: nc.vector.tensor_scalar(
                            out=self.dst_[:, sl, ci * 3 + k, :], in0=self.identb[:, :],
                            scalar1=self.pv[:, l, FW + j * 3 + k:FW + j * 3 + k + 1], scalar2=None, op0=ALU.mult),
                            reads=self.identbb + self.pvb, writes=[self.dstb[sl]])
                S.dma("sp", f"scr{sl}", lambda l=l, q=q, sl=sl: sp.dma_start(
                    out=self.d2_d[l, :, q * 24:(q + 1) * 24, :], in_=self.dst_[:, sl, 0:24, :]),
                    reads=[self.dstb[sl]])
        for k in ("scr0", "scr1"):
            nc.sync.wait_ge(S.sems[k], S.cnt[k])
        self.x_loaded_for = None

    def plan_pieces(self):
        self.pidx = {}
        n_tiles = 1 + self.n_main
        for ti in range(n_tiles):
            for l in self.layers:
                P = {}
                win = self.w_in_d[l]

                def wk(src2d, c0, n, p=128):
                    return src2d[:, c0:c0 + n].rearrange("(k p) n -> p k n", p=p)
                P["vg"] = [self.piece("pool", wk(win, i * 512, 512), 128, 8, 512) for i in range(3)]
                P["u"] = [self.piece("pool", wk(win, 1536 + i * 384, 384), 128, 8, 384) for i in range(2)]
                P["v"] = [self.piece("pool", wk(win, 2304 + i * 384, 384), 128, 8, 384) for i in range(2)]
                P["d1"] = [self.piece("sp", self.d1_d[l, :, c * CK:(c + 1) * CK, :], 128, CK, 128) for c in range(6)]
                P["p"] = [self.piece("pool", wk(win, 3072 + i * 384, 384), 128, 8, 384) for i in range(2)]
                P["gate"] = {}
                P["proj"] = {}
                for br in (1, 2, 0):
                    for mg in range(2):
                        P["gate"][(br, mg)] = self.piece("pool", wk(win, 3840 + br * 1024 + mg * 512, 512), 128, 8, 512)
                        if br == 0:
                            P["proj"][(br, mg)] = self.piece("pool", wk(self.cproj_d[l], mg * 512, 512), 128, 6, 512)
                        elif br == 1:
                            P["proj"][(br, mg)] = self.piece("pool", wk(self.gproj_d[l], mg * 512, 512, 96), 96, 8, 512)
                        else:
                            P["proj"][(br, mg)] = self.piece("pool", wk(self.pproj_d[l], mg * 512, 512, 96), 96, 8, 512)
                P["wout"] = [self.piece("pool", wk(self.wout_d[l], i * 512, 512), 128, 8, 512) for i in range(2)]
                P["ffn"] = []
                for q in range(4):
                    g = self.piece("pool", wk(self.wup_d[l], q * 512, 512), 128, 8, 512)
                    v = self.piece("pool", wk(self.wup_d[l], DFF + q * 512, 512), 128, 8, 512)
                    d2 = self.piece("sp", self.d2_d[l, :, q * 24:(q + 1) * 24, :], 128, 24, 128)
                    P["ffn"].append((g, v, d2))
                P["wdown"] = [self.piece("pool", wk(self.wdown_d[l], i * 256, 256), 128, 16, 256) for i in range(4)]
                self.pidx[(ti, l)] = P

    def load_x(self, ti):
        nc, S = self.nc, self.S
        t0, T = ([(0, HALO)] + [(HALO + i * TT, TT) for i in range(self.n_main)])[ti]
        nb = T // 128
        S.dma("sp", "xin", lambda: nc.sync.dma_start(
            out=self.xstage[:, 0:nb, :], in_=self.x_d[t0:t0 + T, :].rearrange("(b p) d -> p b d", p=128)),
            writes=self.xstageb[0:nb])
        self.x_loaded_for = ti

    def transpose_in(self):
        nc, S = self.nc, self.S
        i = 0
        for b in range(self.nb):
            for g in range(2):
                pt, pb = self.bank()
                S.group("pe", [(lambda j=j: nc.tensor.transpose(pt[:, j * 128:(j + 1) * 128],
                                                                self.xstage[:, b, (4 * g + j) * 128:(4 * g + j + 1) * 128],
                                                                self.identf[:, :])) for j in range(4)],
                        reads=[self.xstageb[b]] + self.identfb, writes=[pb])
                outv = self.xT[:, 4 * g:4 * g + 4, b * 128:(b + 1) * 128]
                inv = pt[:, :].rearrange("p (j t) -> p j t", j=4)
                if i % 2 == 0:
                    S.op("act", lambda outv=outv, inv=inv: nc.scalar.copy(out=outv, in_=inv),
                         reads=[pb], writes=self.xTb[4 * g:4 * g + 4])
                else:
                    S.op("dve", lambda outv=outv, inv=inv: nc.vector.tensor_copy(out=outv, in_=inv),
                         reads=[pb], writes=self.xTb[4 * g:4 * g + 4])
                i += 1

    def rmsnorm(self, l, gcol):
        nc, S, T = self.nc, self.S, self.T
        for g in range(2):
            S.op("act", lambda g=g: nc.scalar.activation(out=self.sqb[:, 4 * g:4 * g + 4, 0:T],
                                                         in_=self.xT[:, 4 * g:4 * g + 4, 0:T], func=AF.Square),
                 reads=self.xTb[4 * g:4 * g + 4], writes=self.sqbb[4 * g:4 * g + 4])
        ps, pb = self.bank()
        S.group("pe", [(lambda k=k: nc.tensor.matmul(ps[:, 0:T], lhsT=self.onesb[:, :], rhs=self.sqb[:, k, 0:T],
                                                     start=(k == 0), stop=(k == 7))) for k in range(8)],
                reads=self.sqbb[0:8] + self.onesbb, writes=[pb])
        S.op("act", lambda: nc.scalar.activation(out=self.rt[:, 0, 0:T], in_=ps[:, 0:T], func=AF.Sqrt,
                                                 bias=self.epst[:, :], scale=1.0 / D),
             reads=[pb], writes=[self.rtb[0]])
        S.op("dve", lambda: nc.vector.reciprocal(out=self.rt[:, 1, 0:T], in_=self.rt[:, 0, 0:T]),
             reads=[self.rtb[0]], writes=[self.rtb[1]])
        for k in range(8):
            S.op("dve", lambda k=k: nc.vector.scalar_tensor_tensor(
                out=self.hT[:, k, 0:T], in0=self.xT[:, k, 0:T], scalar=self.pv[:, l, gcol + k:gcol + k + 1],
                in1=self.rt[:, 1, 0:T], op0=ALU.mult, op1=ALU.mult),
                reads=[self.xTb[k], self.rtb[1]] + self.pvb, writes=[self.hTb[k]])

    def mixer(self, l):
        nc, S, T, nb = self.nc, self.S, self.T, self.nb
        P = self.pidx[(self.ti, l)]
        O_G1, O_G2, O_BCV, O_BCG, O_BU, O_BG, O_CLG, O_CLB, O_GLG, O_GLB, O_PSC = \
            0, 8, 16, 22, 28, 36, 60, 66, 72, 80, 88
        pv, pvh = self.pv, self.pvh
        self.rmsnorm(l, O_G1)
        hTb = self.hTb
        self.chk(2)

        S.op("act", lambda: nc.scalar.copy(out=self.abuf[:, :, 0:30], in_=self.ahist[:, l, :, 0:30]),
             reads=self.ahistb, writes=self.abufb)
        vg = [self.use(i) for i in P["vg"]]
        for c in range(6):
            pvv, pvb_ = self.bank()
            pgg, pgb_ = self.bank()
            cv = c * 128
            cg = 768 + c * 128
            wv, wvb = vg[cv // 512]
            wg, wgb = vg[cg // 512]
            S.group("pe", [(lambda k=k: nc.tensor.matmul(pvv[:, 0:T], lhsT=wv[:, k, cv % 512:cv % 512 + 128],
                                                         rhs=self.hT[:, k, 0:T], start=(k == 0), stop=(k == 7)))
                           for k in range(8)], reads=hTb + [wvb], writes=[pvb_])
            S.group("pe", [(lambda k=k: nc.tensor.matmul(pgg[:, 0:T], lhsT=wg[:, k, cg % 512:cg % 512 + 128],
                                                         rhs=self.hT[:, k, 0:T], start=(k == 0), stop=(k == 7)))
                           for k in range(8)], reads=hTb + [wgb], writes=[pgb_])
            s0, s1 = (c % 2) * 2, (c % 2) * 2 + 1
            S.op("act", lambda c=c, s0=s0: nc.scalar.activation(out=self.tmpA[:, s0, 0:T], in_=pvv[:, 0:T],
                                                                func=AF.Identity, bias=pv[:, l, O_BCV + c:O_BCV + c + 1],
                                                                scale=1.0),
                 reads=[pvb_] + self.pvb, writes=[self.tmpAb[s0]])
            S.op("act", lambda c=c, s1=s1: nc.scalar.activation(out=self.tmpA[:, s1, 0:T], in_=pgg[:, 0:T],
                                                                func=AF.Tanh, bias=pvh[:, l, O_BCG + c:O_BCG + c + 1],
                                                                scale=0.5),
                 reads=[pgb_] + self.pvhb, writes=[self.tmpAb[s1]])
            S.op("dve", lambda c=c, s0=s0, s1=s1: nc.vector.scalar_tensor_tensor(
                out=self.abuf[:, c, 30:30 + T], in0=self.tmpA[:, s1, 0:T], scalar=1.0, in1=self.tmpA[:, s0, 0:T],
                op0=ALU.add, op1=ALU.mult), reads=[self.tmpAb[s0], self.tmpAb[s1]], writes=[self.abufb[c]])
        S.op("act", lambda: nc.scalar.copy(out=self.ahist[:, l, :, 0:30], in_=self.abuf[:, :, T:T + 30]),
             reads=self.abufb, writes=self.ahistb)
        if self.ti == 0:
            S.op("dve", lambda: nc.vector.tensor_scalar(out=self.ahist[:, l, :, :], in0=self.ahist[:, l, :, :],
                                                        scalar1=self.maskt[:, 0:1], scalar2=None, op0=ALU.mult),
                 reads=self.ahistb + self.masktb, writes=self.ahistb)

        self.chk(3)
        up = [self.use(i) for i in P["u"]]
        for c in range(8):
            pu, pub = self.bank()
            cc = c * 96
            wu, wub = up[cc // 384]
            S.group("pe", [(lambda k=k: nc.tensor.matmul(pu[0:96, 0:T], lhsT=wu[:, k, cc % 384:cc % 384 + 96],
                                                         rhs=self.hT[:, k, 0:T], start=(k == 0), stop=(k == 7)))
                           for k in range(8)], reads=hTb + [wub], writes=[pub])
            S.op("act", lambda c=c: nc.scalar.activation(out=self.uT[:, c, 0:T], in_=pu[0:96, 0:T],
                                                         func=AF.Gelu_apprx_tanh,
                                                         bias=pv[0:96, l, O_BU + c:O_BU + c + 1], scale=1.0),
                 reads=[pub] + self.pvb, writes=[self.uTb[c]])

        self.chk(4)
        vp = [self.use(i) for i in P["v"]]
        S.op("dve", lambda: nc.vector.memset(self.stat[:, :], 0.0), writes=self.statb)
        for b in range(nb):
            for hh in range(2):
                pvv, pvb_ = self.bank()
                wv, wvb = vp[hh]
                fns = [(lambda k=k: nc.tensor.matmul(pvv[:, 0:384], lhsT=self.hT[:, k, b * 128:(b + 1) * 128],
                                                     rhs=wv[:, k, :], start=(k == 0), stop=False)) for k in range(8)]
                fns.append(lambda hh=hh: nc.tensor.matmul(pvv[:, 0:384], lhsT=self.onesb[0:1, :],
                                                          rhs=self.brow[0:1, l, hh * 384:(hh + 1) * 384],
                                                          start=False, stop=True))
                S.group("pe", fns, reads=hTb + [wvb] + self.browb + self.onesbb, writes=[pvb_])
                S.op("act", lambda b=b, hh=hh: nc.scalar.activation(
                    out=self.gv[:, b, hh * 384:(hh + 1) * 384], in_=pvv[:, 0:384], func=AF.Gelu_apprx_tanh,
                    accum_out=self.stat[:, b * 2 + hh:b * 2 + hh + 1]),
                    reads=[pvb_], writes=[self.gvb[b]] + self.statb)
            S.op("act", lambda b=b: nc.scalar.activation(
                out=self.vn[:, b, :], in_=self.gv[:, b, :], func=AF.Square, accum_out=self.stat[:, 8 + b:9 + b]),
                reads=[self.gvb[b]], writes=[self.vnb[b]] + self.statb)
        st = self.stat
        sv = st[:, 0:8].rearrange("p (b h) -> p b h", h=2)
        S.op("dve", lambda: nc.vector.tensor_tensor(out=st[:, 16:16 + nb], in0=sv[:, 0:nb, 0], in1=sv[:, 0:nb, 1],
                                                    op=ALU.add), reads=self.statb, writes=self.statb)
        S.op("dve", lambda: nc.vector.tensor_scalar(out=st[:, 16:16 + nb], in0=st[:, 16:16 + nb], scalar1=1.0 / DB,
                                                    scalar2=None, op0=ALU.mult), reads=self.statb, writes=self.statb)
        S.op("dve", lambda: nc.vector.tensor_tensor(out=st[:, 20:20 + nb], in0=st[:, 16:16 + nb],
                                                    in1=st[:, 16:16 + nb], op=ALU.mult),
             reads=self.statb, writes=self.statb)
        S.op("dve", lambda: nc.vector.scalar_tensor_tensor(out=st[:, 20:20 + nb], in0=st[:, 8:8 + nb],
                                                           scalar=1.0 / DB, in1=st[:, 20:20 + nb],
                                                           op0=ALU.mult, op1=ALU.subtract),
             reads=self.statb, writes=self.statb)
        S.op("act", lambda: nc.scalar.activation(out=st[:, 24:24 + nb], in_=st[:, 20:20 + nb], func=AF.Sqrt,
                                                 bias=self.epst[:, :], scale=1.0), reads=self.statb, writes=self.statb)
        S.op("dve", lambda: nc.vector.reciprocal(out=st[:, 24:24 + nb], in_=st[:, 24:24 + nb]),
             reads=self.statb, writes=self.statb)
        S.op("dve", lambda: nc.vector.scalar_tensor_tensor(out=st[:, 28:28 + nb], in0=st[:, 16:16 + nb], scalar=-1.0,
                                                           in1=st[:, 24:24 + nb], op0=ALU.mult, op1=ALU.mult),
             reads=self.statb, writes=self.statb)
        for b in range(nb):
            S.op("act", lambda b=b: nc.scalar.activation(out=self.vn[:, b, :], in_=self.gv[:, b, :], func=AF.Identity,
                                                         bias=st[:, 28 + b:29 + b], scale=st[:, 24 + b:25 + b]),
                 reads=[self.gvb[b]] + self.statb, writes=[self.vnb[b]])

        self.chk(5)
        for c in range(6):
            d1, d1b = self.use(P["d1"][c])
            pc, pcb = self.bank()
            S.group("pe", [(lambda k=k: nc.tensor.matmul(pc[:, 0:T], lhsT=d1[:, k, :], rhs=self.abuf[:, c, k:k + T],
                                                         start=(k == 0), stop=(k == CK - 1))) for k in range(CK)],
                    reads=[self.abufb[c], d1b], writes=[pcb])
            S.op("act", lambda c=c: nc.scalar.copy(out=self.cbuf[:, c, 0:T], in_=pc[:, 0:T]),
                 reads=[pcb], writes=[self.cbufb[c]])
            S.op("act", lambda c=c: nc.scalar.copy(out=self.sqb[:, c, 0:T], in_=pc[:, 0:T]),
                 reads=[pcb], writes=[self.sqbb[c]])
            S.op("act", lambda c=c: nc.scalar.activation(out=self.sqb[:, 6 + c, 0:T], in_=pc[:, 0:T], func=AF.Square),
                 reads=[pcb], writes=[self.sqbb[6 + c]])
        p1, p1b = self.bank()
        p2, p2b = self.bank()
        S.group("pe", [(lambda c=c: nc.tensor.matmul(p1[:, 0:T], lhsT=self.onesb[:, :], rhs=self.sqb[:, c, 0:T],
                                                     start=(c == 0), stop=(c == 5))) for c in range(6)],
                reads=self.sqbb[0:6] + self.onesbb, writes=[p1b])
        S.group("pe", [(lambda c=c: nc.tensor.matmul(p2[:, 0:T], lhsT=self.onesb[:, :], rhs=self.sqb[:, 6 + c, 0:T],
                                                     start=(c == 0), stop=(c == 5))) for c in range(6)],
                reads=self.sqbb[6:12] + self.onesbb, writes=[p2b])
        rt, rtb = self.rt, self.rtb
        S.op("dve", lambda: nc.vector.tensor_scalar(out=rt[:, 0, 0:T], in0=p1[:, 0:T], scalar1=1.0 / DB, scalar2=None,
                                                    op0=ALU.mult), reads=[p1b], writes=[rtb[0]])
        S.op("dve", lambda: nc.vector.tensor_tensor(out=rt[:, 1, 0:T], in0=rt[:, 0, 0:T], in1=rt[:, 0, 0:T],
                                                    op=ALU.mult), reads=[rtb[0]], writes=[rtb[1]])
        S.op("dve", lambda: nc.vector.scalar_tensor_tensor(out=rt[:, 2, 0:T], in0=p2[:, 0:T], scalar=1.0 / DB,
                                                           in1=rt[:, 1, 0:T], op0=ALU.mult, op1=ALU.subtract),
             reads=[p2b, rtb[1]], writes=[rtb[2]])
        S.op("act", lambda: nc.scalar.activation(out=rt[:, 1, 0:T], in_=rt[:, 2, 0:T], func=AF.Sqrt,
                                                 bias=self.epst[:, :], scale=1.0), reads=[rtb[2]], writes=[rtb[1]])
        S.op("dve", lambda: nc.vector.reciprocal(out=rt[:, 2, 0:T], in_=rt[:, 1, 0:T]),
             reads=[rtb[1]], writes=[rtb[2]])
        S.op("dve", lambda: nc.vector.scalar_tensor_tensor(out=rt[:, 0, 0:T], in0=rt[:, 0, 0:T], scalar=-1.0,
                                                           in1=rt[:, 2, 0:T], op0=ALU.mult, op1=ALU.mult),
             reads=[rtb[0], rtb[2]], writes=[rtb[0]])
        for c in range(6):
            s0, s1 = (c % 2) * 2, (c % 2) * 2 + 1
            gh = pvh[:, l, O_CLG + c:O_CLG + c + 1]
            bh = pvh[:, l, O_CLB + c:O_CLB + c + 1]
            S.op("dve", lambda c=c, gh=gh, s0=s0: nc.vector.scalar_tensor_tensor(
                out=self.tmpA[:, s0, 0:T], in0=self.cbuf[:, c, 0:T], scalar=gh, in1=rt[:, 2, 0:T],
                op0=ALU.mult, op1=ALU.mult), reads=[self.cbufb[c], rtb[2]] + self.pvhb, writes=[self.tmpAb[s0]])
            S.op("dve", lambda c=c, gh=gh, s0=s0: nc.vector.scalar_tensor_tensor(
                out=self.tmpA[:, s0, 0:T], in0=rt[:, 0, 0:T], scalar=gh, in1=self.tmpA[:, s0, 0:T],
                op0=ALU.mult, op1=ALU.add), reads=[self.tmpAb[s0], rtb[0]] + self.pvhb, writes=[self.tmpAb[s0]])
            S.op("act", lambda c=c, bh=bh, s0=s0, s1=s1: nc.scalar.activation(
                out=self.tmpA[:, s1, 0:T], in_=self.tmpA[:, s0, 0:T], func=AF.Tanh, bias=bh, scale=1.0),
                reads=[self.tmpAb[s0]] + self.pvhb, writes=[self.tmpAb[s1]])
            S.op("act", lambda c=c, bh=bh, s0=s0: nc.scalar.activation(
                out=self.cbuf[:, c, 0:T], in_=self.tmpA[:, s0, 0:T], func=AF.Identity, bias=bh, scale=1.0),
                reads=[self.tmpAb[s0]] + self.pvhb, writes=[self.cbufb[c]])
            S.op("dve", lambda c=c, s1=s1: nc.vector.scalar_tensor_tensor(
                out=self.sT[:, c, 0:T], in0=self.tmpA[:, s1, 0:T], scalar=1.0, in1=self.cbuf[:, c, 0:T],
                op0=ALU.add, op1=ALU.mult), reads=[self.tmpAb[s1], self.cbufb[c]], writes=[self.sTb[c]])

        self.chk(6)
        for c in range(8):
            h = c // 2
            pz, pzb = self.bank()
            S.group("pe", [(lambda b=b: nc.tensor.matmul(pz[0:96, b * 128:(b + 1) * 128],
                                                         lhsT=self.vn[:, b, c * 96:(c + 1) * 96],
                                                         rhs=self.wsT[:, l, h, :], start=True, stop=True))
                           for b in range(nb)], reads=self.vnb[0:nb] + self.wsTb, writes=[pzb])
            s = c % 2
            S.op("dve", lambda c=c, s=s: nc.vector.scalar_tensor_tensor(
                out=self.tmpB[0:96, s, 0:T].rearrange("p (b i) -> p b i", i=128),
                in0=pz[0:96, 0:T].rearrange("p (b i) -> p b i", i=128),
                scalar=pv[0:96, l, O_GLG + c:O_GLG + c + 1],
                in1=self.Cm[:, l, c:c + 1, :].to_broadcast([96, nb, 128]),
                op0=ALU.mult, op1=ALU.add), reads=[pzb] + self.pvb + self.Cmb, writes=[self.tmpBb[s]])
            S.op("dve", lambda c=c, s=s: nc.vector.tensor_tensor(out=self.uT[:, c, 0:T], in0=self.tmpB[0:96, s, 0:T],
                                                                  in1=self.uT[:, c, 0:T], op=ALU.mult),
                 reads=[self.tmpBb[s], self.uTb[c]], writes=[self.uTb[c]])

        self.chk(7)
        S.op("act", lambda: nc.scalar.copy(out=self.zpt[:, 0, :], in_=self.zprev[:, l, :]),
             reads=self.zprevb, writes=[self.zptb[0]])
        pp = [self.use(i) for i in P["p"]]
        for b in range(nb):
            for hh in range(2):
                pz, pzb = self.bank()
                wp, wpb = pp[hh]
                fns = [(lambda k=k: nc.tensor.matmul(pz[:, 0:384], lhsT=self.hT[:, k, b * 128:(b + 1) * 128],
                                                     rhs=wp[:, k, :], start=(k == 0), stop=False)) for k in range(8)]
                fns.append(lambda hh=hh: nc.tensor.matmul(pz[:, 0:384], lhsT=self.onesb[0:1, :],
                                                          rhs=self.brow[0:1, l, DB + hh * 384:DB + (hh + 1) * 384],
                                                          start=False, stop=True))
                S.group("pe", fns, reads=hTb + [wpb] + self.browb + self.onesbb, writes=[pzb])
                S.op("act", lambda b=b, hh=hh: nc.scalar.copy(out=self.zpt[:, 1 + b, hh * 384:(hh + 1) * 384],
                                                              in_=pz[:, 0:384]),
                     reads=[pzb], writes=[self.zptb[1 + b]])
        S.op("act", lambda: nc.scalar.copy(out=self.zprev[:, l, :], in_=self.zpt[:, nb, :]),
             reads=[self.zptb[nb]], writes=self.zprevb)
        for c in range(8):
            g = c // 2
            pz, pzb = self.bank()
            fns = []
            for b in range(nb):
                first = (self.ti == 1 and b == 0)
                kd, ko = (2, 3) if first else (0, 1)
                fns.append(lambda b=b, kd=kd: nc.tensor.matmul(pz[0:96, b * 128:(b + 1) * 128],
                                                               lhsT=self.zpt[:, 1 + b, c * 96:(c + 1) * 96],
                                                               rhs=self.band[:, g * 4 + kd, :], start=True, stop=False))
                fns.append(lambda b=b, ko=ko: nc.tensor.matmul(pz[0:96, b * 128:(b + 1) * 128],
                                                               lhsT=self.zpt[:, b, c * 96:(c + 1) * 96],
                                                               rhs=self.band[:, g * 4 + ko, :], start=False, stop=True))
            S.group("pe", fns, reads=self.zptb[0:nb + 1] + self.bandb, writes=[pzb])
            S.op("act", lambda c=c: nc.scalar.copy(out=self.pooledT[:, c, 0:T], in_=pz[0:96, 0:T]),
                 reads=[pzb], writes=[self.pooledTb[c]])
        for g in range(4):
            pms = []
            for oc in range(2):
                pm, pmb = self.bank()
                S.group("pe", [(lambda kc=kc: nc.tensor.matmul(
                    pm[0:96, 0:T], lhsT=self.poolw[:, l, g * 2 + kc, oc * 96:(oc + 1) * 96],
                    rhs=self.pooledT[:, g * 2 + kc, 0:T], start=(kc == 0), stop=(kc == 1))) for kc in range(2)],
                    reads=self.pooledTb[g * 2:g * 2 + 2] + self.poolwb, writes=[pmb])
                pms.append((pm, pmb))
            for oc in range(2):
                pm, pmb = pms[oc]
                c = g * 2 + oc
                S.op("act", lambda c=c, pm=pm: nc.scalar.activation(
                    out=self.pooledT[:, c, 0:T], in_=pm[0:96, 0:T], func=AF.Identity,
                    scale=pv[0:96, l, O_PSC + c:O_PSC + c + 1]),
                    reads=[pmb] + self.pvb, writes=[self.pooledTb[c]])

        self.chk(8)
        order = (1, 2, 0)
        srcs = {0: (self.sT, self.sTb, 6, 128), 1: (self.uT, self.uTb, 8, 96), 2: (self.pooledT, self.pooledTb, 8, 96)}
        for bi, br in enumerate(order):
            src, srcb, nk, kp = srcs[br]
            for mg in range(2):
                wg, wgb = self.use(P["gate"][(br, mg)])
                wp, wpb = self.use(P["proj"][(br, mg)])
                for mm in range(4):
                    m = mg * 4 + mm
                    pg, pgb = self.bank()
                    ph, phb = self.bank()
                    S.group("pe", [(lambda k=k: nc.tensor.matmul(pg[:, 0:T], lhsT=wg[:, k, mm * 128:(mm + 1) * 128],
                                                                 rhs=self.hT[:, k, 0:T], start=(k == 0), stop=(k == 7)))
                                   for k in range(8)], reads=hTb + [wgb], writes=[pgb])
                    S.group("pe", [(lambda k=k: nc.tensor.matmul(ph[:, 0:T], lhsT=wp[0:kp, k, mm * 128:(mm + 1) * 128],
                                                                 rhs=src[0:kp, k, 0:T], start=(k == 0),
                                                                 stop=(k == nk - 1))) for k in range(nk)],
                            reads=srcb + [wpb], writes=[phb])
                    s = m % 2
                    bcol = O_BG + br * 8 + m
                    S.op("act", lambda s=s, bcol=bcol: nc.scalar.activation(
                        out=self.thg[:, s, 0:T], in_=pg[:, 0:T], func=AF.Tanh, bias=pvh[:, l, bcol:bcol + 1],
                        scale=0.5), reads=[pgb] + self.pvhb, writes=[self.thgb[s]])
                    if bi == 0:
                        S.op("dve", lambda s=s, m=m: nc.vector.scalar_tensor_tensor(
                            out=self.yacc[:, m, 0:T], in0=self.thg[:, s, 0:T], scalar=1.0, in1=ph[:, 0:T],
                            op0=ALU.add, op1=ALU.mult), reads=[self.thgb[s], phb], writes=[self.yaccb[m]])
                    else:
                        S.op("dve", lambda s=s, m=m: nc.vector.scalar_tensor_tensor(
                            out=self.tg[:, s, 0:T], in0=self.thg[:, s, 0:T], scalar=1.0, in1=ph[:, 0:T],
                            op0=ALU.add, op1=ALU.mult), reads=[self.thgb[s], phb], writes=[self.tgb[s]])
                        if bi == 1:
                            S.op("dve", lambda s=s, m=m: nc.vector.tensor_tensor(
                                out=self.yacc[:, m, 0:T], in0=self.yacc[:, m, 0:T], in1=self.tg[:, s, 0:T],
                                op=ALU.add), reads=[self.tgb[s], self.yaccb[m]], writes=[self.yaccb[m]])
                        else:
                            S.op("dve", lambda s=s, m=m: nc.vector.tensor_tensor(
                                out=self.yT[:, m, 0:T], in0=self.yacc[:, m, 0:T], in1=self.tg[:, s, 0:T],
                                op=ALU.add), reads=[self.tgb[s], self.yaccb[m]], writes=[self.yTb[m]])

        self.chk(9)
        wo = [self.use(i) for i in P["wout"]]
        for m in range(8):
            w, wb = wo[m // 4]
            po, pob = self.bank()
            S.group("pe", [(lambda k=k: nc.tensor.matmul(po[:, 0:T], lhsT=w[:, k, (m % 4) * 128:(m % 4 + 1) * 128],
                                                         rhs=self.yT[:, k, 0:T], start=(k == 0), stop=(k == 7)))
                           for k in range(8)], reads=self.yTb + [wb], writes=[pob])
            S.op("dve", lambda m=m, po=po: nc.vector.scalar_tensor_tensor(
                out=self.xT[:, m, 0:T], in0=po[:, 0:T], scalar=0.5, in1=self.xT[:, m, 0:T],
                op0=ALU.mult, op1=ALU.add), reads=[pob, self.xTb[m]], writes=[self.xTb[m]])

    def ffn(self, l):
        nc, S, T = self.nc, self.S, self.T
        P = self.pidx[(self.ti, l)]
        self.chk(10)
        self.rmsnorm(l, 8)
        hTb = self.hTb
        par = self.ti % 2
        if l == self.layers[-1] and self.next_tile is not None and not self.do_final_here():
            pass
        pend = None

        def stageB(j, q, jj, d2, d2b, pcg, pcgb, pcv, pcvb):
            s = j % 2
            S.op("act", lambda: nc.scalar.activation(out=self.gg[:, s, 0:T], in_=pcg[:, 0:T],
                                                     func=AF.Gelu_apprx_tanh), reads=[pcgb], writes=[self.ggb[s]])
            S.op("dve", lambda: nc.vector.tensor_tensor(out=self.actT[:, j, 0:T], in0=self.gg[:, s, 0:T],
                                                        in1=pcv[:, 0:T], op=ALU.mult),
                 reads=[self.ggb[s], pcvb], writes=[self.actTb[j]])

        for q in range(4):
            ig, iv, id2 = P["ffn"][q]
            wg, wgb = self.use(ig)
            wv, wvb = self.use(iv)
            d2, d2b = self.use(id2)
            for jj in range(4):
                j = 4 * q + jj
                outs = []
                for which, (w, wb) in enumerate(((wg, wgb), (wv, wvb))):
                    ch = j if which == 0 else 16 + j
                    pu, pub = self.bank()
                    S.group("pe", [(lambda k=k, w=w: nc.tensor.matmul(pu[:, 0:T], lhsT=w[:, k, jj * 128:(jj + 1) * 128],
                                                                      rhs=self.hT[:, k, 0:T], start=(k == 0),
                                                                      stop=(k == 7))) for k in range(8)],
                            reads=hTb + [wb], writes=[pub])
                    hs = (2 * j + which) % 4
                    S.op("act", lambda hs=hs, pu=pu: nc.scalar.copy(out=self.hup[:, hs, 0:T], in_=pu[:, 0:T]),
                         reads=[pub], writes=[self.hupb[hs]])
                    S.op("act", lambda ch=ch, pu=pu: nc.scalar.copy(out=self.fhist[:, l, 1 - par, ch, :],
                                                                    in_=pu[:, T - 2:T]),
                         reads=[pub], writes=[self.fhistb[l * 2 + 1 - par]])
                    outs.append((hs, ch))
                pcs = []
                for which, (hs, ch) in enumerate(outs):
                    pc, pcb = self.bank()
                    di = (which * 4 + jj) * 3
                    hv = self.hup[:, hs, :]
                    fh = self.fhist[:, l, par, ch, :]
                    fns = [
                        lambda hv=hv, di=di, pc=pc: nc.tensor.matmul(pc[:, 0:T], lhsT=d2[:, di + 2, :], rhs=hv[:, 0:T],
                                                                     start=True, stop=False),
                        lambda hv=hv, di=di, pc=pc: nc.tensor.matmul(pc[:, 1:T], lhsT=d2[:, di + 1, :],
                                                                     rhs=hv[:, 0:T - 1], start=False, stop=False),
                        lambda fh=fh, di=di, pc=pc: nc.tensor.matmul(pc[:, 0:1], lhsT=d2[:, di + 1, :], rhs=fh[:, 1:2],
                                                                     start=False, stop=False),
                        lambda hv=hv, di=di, pc=pc: nc.tensor.matmul(pc[:, 2:T], lhsT=d2[:, di + 0, :],
                                                                     rhs=hv[:, 0:T - 2], start=False, stop=False),
                        lambda fh=fh, di=di, pc=pc: nc.tensor.matmul(pc[:, 0:2], lhsT=d2[:, di + 0, :], rhs=fh[:, 0:2],
                                                                     start=False, stop=True),
                    ]
                    S.group("pe", fns, reads=[self.hupb[hs], d2b, self.fhistb[l * 2 + par]], writes=[pcb])
                    pcs.append((pc, pcb))
                if pend is not None:
                    stageB(*pend)
                pend = (j, q, jj, d2, d2b, pcs[0][0], pcs[0][1], pcs[1][0], pcs[1][1])
        stageB(*pend)
        if self.ti == 0:
            S.op("dve", lambda: nc.vector.tensor_scalar(out=self.fhist[:, l, 1 - par, :, :],
                                                        in0=self.fhist[:, l, 1 - par, :, :],
                                                        scalar1=self.maskt[:, 0:1], scalar2=None, op0=ALU.mult),
                 reads=[self.fhistb[l * 2 + 1 - par]] + self.masktb, writes=[self.fhistb[l * 2 + 1 - par]])
        self.chk(11)
        for m in range(8):
            w, wb = self.use(P["wdown"][m // 2], look=2)
            po, pob = self.bank()
            S.group("pe", [(lambda k=k: nc.tensor.matmul(po[:, 0:T], lhsT=w[:, k, (m % 2) * 128:(m % 2 + 1) * 128],
                                                         rhs=self.actT[:, k, 0:T], start=(k == 0), stop=(k == 15)))
                           for k in range(16)], reads=self.actTb + [wb], writes=[pob])
            S.op("dve", lambda m=m, po=po: nc.vector.tensor_tensor(out=self.xT[:, m, 0:T], in0=po[:, 0:T],
                                                                   in1=self.xT[:, m, 0:T], op=ALU.add),
                 reads=[pob, self.xTb[m]], writes=[self.xTb[m]])

    def do_final_here(self):
        return self.ti >= 1 and self.do_final

    def final(self, tout0):
        nc, S, T, nb = self.nc, self.S, self.T, self.nb
        st = self.stat
        if self.next_tile is not None:
            self.load_x(self.next_tile)
        S.op("dve", lambda: nc.vector.memset(st[:, 0:16], 0.0), writes=self.statb)
        for b in range(nb):
            banks = []
            for g in range(2):
                pt, pb = self.bank()
                S.group("pe", [(lambda j=j: nc.tensor.transpose(pt[:, j * 128:(j + 1) * 128],
                                                                self.xT[:, 4 * g + j, b * 128:(b + 1) * 128],
                                                                self.identf[:, :])) for j in range(4)],
                        reads=self.xTb[4 * g:4 * g + 4] + self.identfb, writes=[pb])
                s = (b * 2 + g) % 2
                S.op("act", lambda b=b, g=g, pt=pt, s=s: nc.scalar.activation(
                    out=self.tmpB[:, s, :], in_=pt[:, :], func=AF.Square, accum_out=st[:, b * 2 + g:b * 2 + g + 1]),
                    reads=[pb], writes=[self.tmpBb[s]] + self.statb)
                banks.append((pt, pb))
            S.op("dve", lambda b=b: nc.vector.tensor_tensor(out=st[:, 16 + b:17 + b], in0=st[:, 2 * b:2 * b + 1],
                                                            in1=st[:, 2 * b + 1:2 * b + 2], op=ALU.add),
                 reads=self.statb, writes=self.statb)
            S.op("act", lambda b=b: nc.scalar.activation(out=st[:, 20 + b:21 + b], in_=st[:, 16 + b:17 + b],
                                                         func=AF.Sqrt, bias=self.epst[:, :], scale=1.0 / D),
                 reads=self.statb, writes=self.statb)
            S.op("dve", lambda b=b: nc.vector.reciprocal(out=st[:, 24 + b:25 + b], in_=st[:, 20 + b:21 + b]),
                 reads=self.statb, writes=self.statb)
            os_ = b % 2
            for g in range(2):
                pt, pb = banks[g]
                S.op("dve", lambda b=b, g=g, pt=pt, os_=os_: nc.vector.scalar_tensor_tensor(
                    out=self.ostage[:, os_, g * 512:(g + 1) * 512], in0=pt[:, :], scalar=st[:, 24 + b:25 + b],
                    in1=self.gfb[:, g * 512:(g + 1) * 512], op0=ALU.mult, op1=ALU.mult),
                    reads=[pb] + self.statb + self.gfbb, writes=[self.ostageb[os_]])
            r0 = tout0 + b * 128
            S.dma("sp", f"yout{os_}", lambda r0=r0, os_=os_: nc.sync.dma_start(out=self.y_d[r0:r0 + 128, :],
                                                                         in_=self.ostage[:, os_, :]),
                  reads=[self.ostageb[os_]])


def _eps_patch(prog_cls):
    orig_setup = prog_cls.setup

    def setup(self):
        self.epst, self.epstb = self.alloc("epst", [128, 1], F32)
        self.S.op("dve", lambda: self.nc.vector.memset(self.epst[:, :], EPS), writes=self.epstb)
        orig_setup(self)
    prog_cls.setup = setup


_eps_patch(Prog)


def _pack_pv(inp, l):
    pv = np.zeros((128, NPV), np.float32)

    def fm(vec, p=128):
        n = vec.shape[0] // p
        out = np.zeros((128, n), np.float32)
        out[:p, :] = vec.reshape(n, p).T
        return out
    b_in = inp["b_in"][l]
    cols = [fm(inp["attn_norm_g"][l]), fm(inp["ffn_norm_g"][l]), fm(b_in[0:768]), fm(b_in[768:1536]),
            fm(b_in[1536:2304], 96), fm(b_in[3840:6912]), fm(inp["conv_ln_g"][l]), fm(inp["conv_ln_b"][l]),
            fm(inp["gmlp_ln_g"][l], 96), fm(inp["gmlp_ln_b"][l], 96), fm(inp["pool_scale"][l], 96)]
    cw = inp["conv_w"][l]
    cols.append(np.ascontiguousarray(cw.reshape(CK, 6, 128).transpose(2, 1, 0)).reshape(128, 6 * CK))
    fw = inp["ffn_conv_w"][l]
    cols.append(np.ascontiguousarray(fw.reshape(3, 32, 128).transpose(2, 1, 0)).reshape(128, 96))
    pv[:, :] = np.concatenate(cols, axis=1)
    return pv


def _bands(seq_start):
    B = np.zeros((128, 4, 4, 128), np.float32)
    tp = np.arange(128)[:, None]
    t = np.arange(128)[None, :]
    eye = (tp == t).astype(np.float32)
    for g, w in enumerate((2, 4, 8, 16)):
        inwin = ((t - tp >= 0) & (t - tp < w)).astype(np.float32)
        off = ((t - (tp - 128)) < w).astype(np.float32)
        B[:, g, 0, :] = inwin / w - eye
        B[:, g, 1, :] = off / w
        if seq_start:
            cnt = np.minimum(t + 1, w).astype(np.float32)
            B[:, g, 2, :] = inwin / cnt - eye
            B[:, g, 3, :] = 0.0
        else:
            B[:, g, 2, :] = B[:, g, 0, :]
            B[:, g, 3, :] = B[:, g, 1, :]
    return B.reshape(128, 16, 128)


_PROG_CACHE = {}
_N_MAIN = 8


def _get_prog(key=("full",)):
    key = (_N_MAIN,)
    if key not in _PROG_CACHE:
        _PROG_CACHE[key] = Prog(n_main=_N_MAIN)
    return _PROG_CACHE[key]


def kernel(**inputs):
    inp = {k: np.asarray(v) for k, v in inputs.items()}
    x = inp["x"].astype(np.float32, copy=False)
    prog = _get_prog()
    ident = np.eye(128, dtype=np.float32)
    blk = np.arange(128) // 64
    gmask = (blk[None, :] <= blk[:, None]).astype(np.float32)
    cst = np.ascontiguousarray(np.stack([ident, gmask], axis=1))
    pv = np.stack([_pack_pv(inp, l) for l in range(2)], axis=0)
    bsb = np.ascontiguousarray(np.broadcast_to(inp["gmlp_bs"][:, None, :, :], (2, 128, 4, 128))).astype(np.float32)
    gfb = np.ascontiguousarray(np.broadcast_to(inp["final_norm_g"][None, :], (128, D))).astype(np.float32)
    shared = {
        "w_in": inp["w_in"], "b_in": inp["b_in"], "conv_proj": inp["conv_proj"], "gmlp_proj": inp["gmlp_proj"],
        "pool_proj": inp["pool_proj"], "w_out": inp["w_out"], "w_up": inp["w_up"], "w_down": inp["w_down"],
        "gmlp_ws": inp["gmlp_ws"], "pool_w": inp["pool_w"], "pv": pv, "bsb": bsb, "gfb": gfb, "cst": cst,
    }
    if _TINY:
        for k, shp in (("w_in", (2, 128, 128)), ("conv_proj", (2, 96, 128)), ("gmlp_proj", (2, 96, 128)),
                       ("pool_proj", (2, 96, 128)), ("w_out", (2, 128, 128)), ("w_up", (2, 128, 256)),
                       ("w_down", (2, 128, 128))):
            shared[k] = np.zeros(shp, np.float32)
    shared = {k: np.ascontiguousarray(v, dtype=np.float32) for k, v in shared.items()}
    in_maps = []
    for c in range(8):
        b, half = c // 2, c % 2
        if half == 0:
            xc = np.concatenate([np.zeros((HALO, D), np.float32), x[b, 0:_N_MAIN * TT]], axis=0)
        else:
            xc = x[b, TOK - HALO:TOK + _N_MAIN * TT]
        m = dict(shared)
        m["x"] = np.ascontiguousarray(xc)
        m["band"] = _bands(half == 0)
        m["mask"] = np.full((128, 1), 0.0 if half == 0 else 1.0, np.float32)
        in_maps.append(m)
    res = run_bass_kernel_spmd(prog.nc, in_maps, core_ids=list(range(8)))
    if _N_MAIN != 8:
        return [res.results[c]["y"] for c in range(8)]
    out = np.empty((4, 2 * TOK, D), np.float32)
    for c in range(8):
        b, half = c // 2, c % 2
        out[b, half * TOK:(half + 1) * TOK] = res.results[c]["y"]
    return out
```
